# Optimizing a Trainium2 kernel written in Bass

```python
import math
import jax, jax.numpy as jnp
from jax import lax
import numpy as np


D_MODEL = 1024
BATCH = 4
SEQ = 8192
DEPTH = 2

MEM_LEN = 256
SSM_WIDTH = D_MODEL // 2
SSM_GROUP = 16
SSM_GROUPS = SSM_WIDTH // SSM_GROUP
SSM_STATE = 64
DIFF_WIDTH = D_MODEL - SSM_WIDTH
DIFF_HEADS = 4
DIFF_V_DIM = DIFF_WIDTH // DIFF_HEADS
DIFF_QK_DIM = DIFF_V_DIM // 2
MIX_WIDTH = SSM_WIDTH + DIFF_WIDTH
IN_PROJ_WIDTH = SSM_WIDTH + 3 * DIFF_WIDTH
XATTN_HEADS = 4
XATTN_HEAD_DIM = D_MODEL // XATTN_HEADS
D_FF = ((8 * D_MODEL // 3 + 127) // 128) * 128
NUM_BUCKETS = 32
MAX_DISTANCE = 128
Q_BLOCK = 128
EPS = 1e-6

kernel_name = "hybrid_s5_diffattn_macaron_decoder"


def rms_norm(x, gain):
    x32 = x.astype(jnp.float32)
    y = x32 * lax.rsqrt(jnp.mean(x32 * x32, axis=-1, keepdims=True) + EPS)
    return (y * gain.astype(jnp.float32)).astype(x.dtype)


def swiglu(x, w_gate, w_up, w_down):
    return (jax.nn.silu(x @ w_gate) * (x @ w_up)) @ w_down


def rel_bucket(rel):
    n = jnp.maximum(rel, 0)
    max_exact = NUM_BUCKETS // 2
    n_f = jnp.maximum(n, 1).astype(jnp.float32)
    large = max_exact + (jnp.log(n_f / max_exact) / math.log(MAX_DISTANCE / max_exact)
                         * (NUM_BUCKETS - max_exact)).astype(jnp.int32)
    large = jnp.minimum(large, NUM_BUCKETS - 1)
    return jnp.where(n < max_exact, n, large)


def s5_scan(u, lam_re, lam_im, b_re, b_im, c_re, c_im, d, log_dt):
    f32 = jnp.float32
    u32 = u.astype(f32)
    lam = lax.complex(jnp.minimum(lam_re.astype(f32), -1e-4), lam_im.astype(f32))
    dt = jnp.exp(log_dt.astype(f32))[:, None]
    lam_bar = jnp.exp(lam * dt)
    b = lax.complex(b_re.astype(f32), b_im.astype(f32))
    b_bar = ((lam_bar - 1.0) / lam)[:, :, None] * b
    bu = jnp.einsum('gph,blgh->blgp', b_bar, u32.astype(jnp.complex64))
    a = jnp.broadcast_to(lam_bar, (1, u.shape[1]) + lam_bar.shape)

    def combine(e_i, e_j):
        a_i, b_i = e_i
        a_j, b_j = e_j
        return a_j * a_i, a_j * b_i + b_j

    _, states = lax.associative_scan(combine, (a, bu), axis=1)
    c = lax.complex(c_re.astype(f32), c_im.astype(f32))
    return jnp.real(jnp.einsum('ghp,blgp->blgh', c, states)) + d.astype(f32) * u32


def diff_attention(q, k, v, lam, rel_bias):
    f32 = jnp.float32
    B, L = q.shape[0], q.shape[1]
    nb = L // Q_BLOCK
    scale = DIFF_QK_DIM ** -0.5
    k32 = k.astype(f32)
    v32 = v.astype(f32)
    q_blocks = q.astype(f32).reshape(B, nb, Q_BLOCK, DIFF_HEADS, 2, DIFF_QK_DIM).transpose(1, 0, 2, 3, 4, 5)
    starts = jnp.arange(nb, dtype=jnp.int32) * Q_BLOCK
    k_pos = jnp.arange(L, dtype=jnp.int32)
    table = rel_bias.astype(f32)

    def block(args):
        qb, start = args
        rel = (start + jnp.arange(Q_BLOCK, dtype=jnp.int32))[:, None] - k_pos[None, :]
        bias = table[rel_bucket(rel)].transpose(2, 0, 1)
        s = jnp.einsum('bqhmd,bkhmd->bhmqk', qb, k32) * scale + bias[None, :, None]
        s = jnp.where(rel >= 0, s, -jnp.inf)
        p = jax.nn.softmax(s, axis=-1)
        a = p[:, :, 0] - lam * p[:, :, 1]
        return jnp.einsum('bhqk,bkhe->bqhe', a, v32)

    out = lax.map(block, (q_blocks, starts))
    return out.transpose(1, 0, 2, 3, 4).reshape(B, L, DIFF_HEADS, DIFF_V_DIM)


def hybrid_mixer(u, w_in, lam_re, lam_im, b_re, b_im, c_re, c_im, d, log_dt, w_glu, b_glu,
                 ssm_out_norm, lq1, lk1, lq2, lk2, subln, w_out, rel_bias, lam_init):
    f32 = jnp.float32
    B, L, _ = u.shape
    z = u @ w_in
    o0 = SSM_WIDTH
    ssm_in = z[..., :o0].reshape(B, L, SSM_GROUPS, SSM_GROUP)
    q = z[..., o0:o0 + DIFF_WIDTH].reshape(B, L, DIFF_HEADS, 2, DIFF_QK_DIM)
    k = z[..., o0 + DIFF_WIDTH:o0 + 2 * DIFF_WIDTH].reshape(B, L, DIFF_HEADS, 2, DIFF_QK_DIM)
    v = z[..., o0 + 2 * DIFF_WIDTH:].reshape(B, L, DIFF_HEADS, DIFF_V_DIM)

    y = s5_scan(ssm_in, lam_re, lam_im, b_re, b_im, c_re, c_im, d, log_dt).reshape(B, L, SSM_WIDTH)
    y = jax.nn.gelu(y).astype(u.dtype) @ w_glu + b_glu
    y_val, y_gate = jnp.split(y, 2, axis=-1)
    y_ssm = rms_norm(y_val * jax.nn.sigmoid(y_gate), ssm_out_norm)

    lam = (jnp.exp(jnp.sum(lq1.astype(f32) * lk1.astype(f32)))
           - jnp.exp(jnp.sum(lq2.astype(f32) * lk2.astype(f32))) + lam_init)
    o = diff_attention(q, k, v, lam, rel_bias)
    o = rms_norm(o, subln) * (1.0 - lam_init)
    y_attn = o.reshape(B, L, DIFF_WIDTH).astype(u.dtype)

    return jnp.concatenate([y_ssm.astype(u.dtype), y_attn], axis=-1) @ w_out


def memory_cross_attention(hq, m, wq, wkv, wo):
    f32 = jnp.float32
    B, L, _ = hq.shape
    q = (hq @ wq).reshape(B, L, XATTN_HEADS, XATTN_HEAD_DIM)
    kv = (m @ wkv).reshape(B, m.shape[1], 2, XATTN_HEADS, XATTN_HEAD_DIM)
    s = jnp.einsum('bqhd,bkhd->bhqk', q.astype(f32), kv[:, :, 0].astype(f32)) * XATTN_HEAD_DIM ** -0.5
    p = jax.nn.softmax(s, axis=-1)
    o = jnp.einsum('bhqk,bkhd->bqhd', p, kv[:, :, 1].astype(f32)).astype(hq.dtype)
    return o.reshape(B, L, D_MODEL) @ wo


def setup_inputs(seed: int = 0) -> dict:
    key = jax.random.key(seed)
    ks = jax.random.split(key, 40)
    f32 = jnp.float32

    def nrm(k, shape, scale):
        return jax.random.normal(k, shape, f32) * scale

    def gain(k, shape):
        return 1.0 + 0.01 * jax.random.normal(k, shape, f32)

    Lr, D, F, G, P, H = DEPTH, D_MODEL, D_FF, SSM_GROUPS, SSM_STATE, SSM_GROUP
    lam_im0 = jnp.broadcast_to(math.pi * jnp.arange(P, dtype=f32), (Lr, G, P))
    return {
        "x": nrm(ks[0], (BATCH, SEQ, D), 1.0),
        "mem": nrm(ks[1], (BATCH, MEM_LEN, D), 1.0),
        "rel_bias": nrm(ks[2], (NUM_BUCKETS, DIFF_HEADS), 0.5),
        "ffn1_norm": gain(ks[3], (Lr, D)),
        "ffn1_w_gate": nrm(ks[4], (Lr, D, F), D ** -0.5),
        "ffn1_w_up": nrm(ks[5], (Lr, D, F), D ** -0.5),
        "ffn1_w_down": nrm(ks[6], (Lr, F, D), F ** -0.5),
        "mix_norm": gain(ks[7], (Lr, D)),
        "w_in": nrm(ks[8], (Lr, D, IN_PROJ_WIDTH), D ** -0.5),
        "ssm_lambda_re": -0.5 + 0.01 * jax.random.normal(ks[9], (Lr, G, P), f32),
        "ssm_lambda_im": lam_im0 + 0.01 * jax.random.normal(ks[10], (Lr, G, P), f32),
        "ssm_b_re": nrm(ks[11], (Lr, G, P, H), (2.0 * H) ** -0.5),
        "ssm_b_im": nrm(ks[12], (Lr, G, P, H), (2.0 * H) ** -0.5),
        "ssm_c_re": nrm(ks[13], (Lr, G, H, P), (2.0 * P) ** -0.5),
        "ssm_c_im": nrm(ks[14], (Lr, G, H, P), (2.0 * P) ** -0.5),
        "ssm_d": nrm(ks[15], (Lr, G, H), 1.0),
        "ssm_log_dt": jax.random.uniform(ks[16], (Lr, G), f32, math.log(1e-3), math.log(1e-1)),
        "ssm_w_glu": nrm(ks[17], (Lr, SSM_WIDTH, 2 * SSM_WIDTH), SSM_WIDTH ** -0.5),
        "ssm_b_glu": nrm(ks[18], (Lr, 2 * SSM_WIDTH), 0.01),
        "ssm_out_norm": gain(ks[19], (Lr, SSM_WIDTH)),
        "diff_lambda_q1": nrm(ks[20], (Lr, DIFF_QK_DIM), 0.1),
        "diff_lambda_k1": nrm(ks[21], (Lr, DIFF_QK_DIM), 0.1),
        "diff_lambda_q2": nrm(ks[22], (Lr, DIFF_QK_DIM), 0.1),
        "diff_lambda_k2": nrm(ks[23], (Lr, DIFF_QK_DIM), 0.1),
        "diff_subln": gain(ks[24], (Lr, DIFF_V_DIM)),
        "w_out": nrm(ks[25], (Lr, MIX_WIDTH, D), MIX_WIDTH ** -0.5),
        "xattn_norm": gain(ks[26], (Lr, D)),
        "mem_norm": gain(ks[27], (Lr, D)),
        "xattn_wq": nrm(ks[28], (Lr, D, D), D ** -0.5),
        "xattn_wkv": nrm(ks[29], (Lr, D, 2 * D), D ** -0.5),
        "xattn_wo": nrm(ks[30], (Lr, D, D), D ** -0.5),
        "ffn2_norm": gain(ks[31], (Lr, D)),
        "ffn2_w_gate": nrm(ks[32], (Lr, D, F), D ** -0.5),
        "ffn2_w_up": nrm(ks[33], (Lr, D, F), D ** -0.5),
        "ffn2_w_down": nrm(ks[34], (Lr, F, D), F ** -0.5),
        "final_norm": gain(ks[35], (D,)),
    }


def reference(x, mem, rel_bias, ffn1_norm, ffn1_w_gate, ffn1_w_up, ffn1_w_down, mix_norm, w_in,
              ssm_lambda_re, ssm_lambda_im, ssm_b_re, ssm_b_im, ssm_c_re, ssm_c_im, ssm_d,
              ssm_log_dt, ssm_w_glu, ssm_b_glu, ssm_out_norm, diff_lambda_q1, diff_lambda_k1,
              diff_lambda_q2, diff_lambda_k2, diff_subln, w_out, xattn_norm, mem_norm, xattn_wq,
              xattn_wkv, xattn_wo, ffn2_norm, ffn2_w_gate, ffn2_w_up, ffn2_w_down, final_norm):
    h = x
    for l in range(DEPTH):
        lam_init = 0.8 - 0.6 * math.exp(-0.3 * l)
        h = h + 0.5 * swiglu(rms_norm(h, ffn1_norm[l]), ffn1_w_gate[l], ffn1_w_up[l], ffn1_w_down[l])
        h = h + hybrid_mixer(rms_norm(h, mix_norm[l]), w_in[l], ssm_lambda_re[l], ssm_lambda_im[l],
                             ssm_b_re[l], ssm_b_im[l], ssm_c_re[l], ssm_c_im[l], ssm_d[l],
                             ssm_log_dt[l], ssm_w_glu[l], ssm_b_glu[l], ssm_out_norm[l],
                             diff_lambda_q1[l], diff_lambda_k1[l], diff_lambda_q2[l],
                             diff_lambda_k2[l], diff_subln[l], w_out[l], rel_bias, lam_init)
        h = h + memory_cross_attention(rms_norm(h, xattn_norm[l]), rms_norm(mem, mem_norm[l]),
                                       xattn_wq[l], xattn_wkv[l], xattn_wo[l])
        h = h + 0.5 * swiglu(rms_norm(h, ffn2_norm[l]), ffn2_w_gate[l], ffn2_w_up[l], ffn2_w_down[l])
    return rms_norm(h, final_norm)
```

```python
import contextlib
import math
import numpy as np
import ml_dtypes
import concourse.bass as bass
import concourse.mybir as mybir
from concourse.bass_utils import run_bass_kernel_spmd

F32 = mybir.dt.float32
BF16 = mybir.dt.bfloat16
ALU = mybir.AluOpType
AF = mybir.ActivationFunctionType
AX = mybir.AxisListType
NPBF = ml_dtypes.bfloat16

SAME_ENG_SYNC = True

D = 1024
DK = 8
FF = 2816
FK = 22
L = 8192
LH = 4096
NT = 1024
SUB = 512
EPS = 1e-6
NCORES = 8


class Op:
    __slots__ = ("eng", "fn", "deps", "dma", "signal", "count", "sem")

    def __init__(self, eng, fn, dma):
        self.eng = eng
        self.fn = fn
        self.dma = dma
        self.deps = []
        self.signal = False
        self.count = 0
        self.sem = None


class Prog:
    ENGS = ("sp", "act", "pe", "dve", "pool")

    def __init__(self):
        self.nc = bass.Bass("TRN2", target_bir_lowering=False)
        self.stack = contextlib.ExitStack()
        self.ops = []
        self.last_write = {}
        self.readers = {}
        self.dma_sems = {}
        self.dma_counts = {}
        self.bar_deps = []
        self.bar_seen = set()

    def barrier(self):
        last = {}
        for o in self.ops:
            last[("dma", o.dma) if o.dma is not None else ("eng", o.eng)] = o
        self.bar_deps = list(last.values())
        self.bar_seen = set()

    def sb(self, name, shape, dtype):
        return self.stack.enter_context(self.nc.sbuf_tensor(name, list(shape), dtype))

    def ps(self, name, shape, dtype=F32):
        return self.stack.enter_context(self.nc.psum_tensor(name, list(shape), dtype))

    def dram(self, name, shape, dtype, kind="Internal"):
        return self.nc.dram_tensor(name, list(shape), dtype, kind=kind)

    def op(self, eng, fn, reads=(), writes=(), dma=None):
        o = Op(eng, fn, dma)
        deps = []
        for t in tuple(reads) + tuple(writes):
            d = self.last_write.get(t)
            if d is not None:
                deps.append(d)
        for t in writes:
            deps.extend(self.readers.get(t, ()))
        if self.bar_deps and eng not in self.bar_seen:
            self.bar_seen.add(eng)
            for d in self.bar_deps:
                if d.dma is None and d.eng == eng:
                    continue
                deps.append(d)
        seen = set()
        for d in deps:
            if id(d) in seen:
                continue
            seen.add(id(d))
            if d.dma is None and d.eng == eng and dma is None:
                if eng == "pe" or not SAME_ENG_SYNC:
                    continue
            o.deps.append(d)
            d.signal = True
        for t in writes:
            self.last_write[t] = o
            self.readers[t] = []
        for t in reads:
            self.readers.setdefault(t, []).append(o)
        self.ops.append(o)
        return o

    def dma(self, queue, out, in_, reads, writes, key, **kw):
        def fn(e, out=out, in_=in_, kw=kw):
            return e.dma_start(out=out, in_=in_, **kw)
        o = self.op(queue, fn, reads, writes, dma=key)
        o.signal = True
        return o

    def emit(self):
        nc = self.nc
        st = self.stack
        eng_sem = {e: st.enter_context(nc.semaphore("s_" + e)) for e in self.ENGS}
        cnt = {e: 0 for e in self.ENGS}
        for o in self.ops:
            if o.dma is not None:
                if o.dma not in self.dma_sems:
                    self.dma_sems[o.dma] = st.enter_context(
                        nc.semaphore("d%d" % len(self.dma_sems)))
                    self.dma_counts[o.dma] = 0
                self.dma_counts[o.dma] += 16
                o.sem = self.dma_sems[o.dma]
                o.count = self.dma_counts[o.dma]
            elif o.signal:
                cnt[o.eng] += 1
                o.sem = eng_sem[o.eng]
                o.count = cnt[o.eng]
        per_eng = {e: [o for o in self.ops if o.eng == e] for e in self.ENGS}
        finals = [(self.dma_sems[k], self.dma_counts[k]) for k in self.dma_sems]
        with nc.Block() as block:
            decos = {"sp": block.sync, "act": block.scalar, "pe": block.tensor,
                     "dve": block.vector, "pool": block.gpsimd}
            for e in self.ENGS:
                ops = per_eng[e]
                if not ops and e != "sp":
                    continue

                def body(eng, ops=ops, e=e):
                    waited = {}
                    for o in ops:
                        need = {}
                        for d in o.deps:
                            k = id(d.sem)
                            if need.get(k, (None, 0))[1] < d.count:
                                need[k] = (d.sem, d.count)
                        for k, (sem, c) in need.items():
                            if waited.get(k, 0) >= c:
                                continue
                            eng.wait_ge(sem, c)
                            waited[k] = c
                        ins = o.fn(eng)
                        if o.sem is not None:
                            ins.then_inc(o.sem, 16 if o.dma is not None else 1)
                    if e == "sp":
                        for sem, c in finals:
                            if waited.get(id(sem), 0) >= c:
                                continue
                            eng.wait_ge(sem, c)
                decos[e](body)
        self.stack.close()


def mkap(base, offset_add, pairs):
    return bass.AP(base.tensor, base.offset + offset_add, [list(p) for p in pairs])


class Ctx:
    pass


def owner_common(P, C):
    C.hT = P.sb("hT", [128, DK, NT], F32)
    C.xn = P.sb("xn", [128, DK, NT], BF16)
    C.sq = P.sb("sq", [128, DK, SUB], BF16)
    C.rs = P.sb("rs", [128, SUB], F32)
    C.rstd = P.sb("rstd", [128, SUB], F32)
    C.act = P.sb("act", [128, FK, NT], BF16)
    C.wgs = [P.sb("wgs%d" % i, [128, DK, 256], BF16) for i in range(2)]
    C.wus = [P.sb("wus%d" % i, [128, DK, 256], BF16) for i in range(2)]
    C.wds = [P.sb("wds%d" % i, [128, FK, 128], BF16) for i in range(2)]
    C.st = [P.sb("st%d" % i, [128, SUB], F32) for i in range(2)]
    C.zst = [P.sb("zst%d" % i, [128, SUB], BF16) for i in range(2)]
    C.ones = P.sb("ones", [128, 128], BF16)
    C.G = [P.ps("G%d" % i, [128, SUB]) for i in range(2)]
    C.U = [P.ps("U%d" % i, [128, SUB]) for i in range(2)]
    C.Dn = [P.ps("Dn%d" % i, [128, SUB]) for i in range(2)]
    C.Nb = P.ps("Nb", [128, SUB])
    C.X7 = P.ps("X7", [128, SUB])
    C.i_wgu = 0
    C.i_wd = 0
    C.i_g = 0
    C.i_d = 0
    C.i_z = 0
    P.op("dve", lambda e: e.memset(C.ones[:], 1.0), writes=[("ones",)])


def emit_norm(P, C, sub, gain_sb, gtok):
    s0 = sub * SUB
    P.op("act", lambda e: e.activation(out=C.sq[:], in_=C.hT[:, :, s0:s0 + SUB], func=AF.Square),
         reads=[("hT", sub)], writes=[("sq",)])
    for dk in range(DK):
        P.op("pe", lambda e, dk=dk: e.matmul(C.Nb[:], lhsT=C.ones[:], rhs=C.sq[:, dk, :],
                                             start=(dk == 0), stop=(dk == DK - 1)),
             reads=[("sq",), ("ones",)], writes=[("Nb",)])
    P.op("act", lambda e: e.activation(out=C.rs[:], in_=C.Nb[:], func=AF.Sqrt,
                                       bias=C.epsb[:, 0:1], scale=1.0 / D),
         reads=[("Nb",), ("epsb",)], writes=[("rs",)])
    P.op("dve", lambda e: e.reciprocal(out=C.rstd[:], in_=C.rs[:]),
         reads=[("rs",)], writes=[("rstd",)])
    for dk in range(DK):
        P.op("dve", lambda e, dk=dk: e.scalar_tensor_tensor(
            out=C.xn[:, dk, s0:s0 + SUB], in0=C.hT[:, dk, s0:s0 + SUB],
            scalar=gain_sb[:, dk:dk + 1], in1=C.rstd[:], op0=ALU.mult, op1=ALU.mult),
            reads=[("hT", sub), ("rstd",), gtok], writes=[("xn", sub)])


def load_wgu(P, C, wg_src, wu_src, nk=DK):
    slot = C.i_wgu % 2
    C.i_wgu += 1
    if wg_src is not None:
        P.dma("pool", C.wgs[slot][:, 0:nk, :].rearrange("p a b -> p (a b)"), wg_src,
              reads=[], writes=[("wgs", slot)], key=("wgs", slot))
    if wu_src is not None:
        P.dma("pool", C.wus[slot][:, 0:nk, :].rearrange("p a b -> p (a b)"), wu_src,
              reads=[], writes=[("wus", slot)], key=("wus", slot))
    return slot


def emit_ffn(P, C, wg_d, wu_d, wd_d, gain_sb, gtok):
    for sub in range(2):
        emit_norm(P, C, sub, gain_sb, gtok)
    for fg in range(11):
        slot = load_wgu(P, C, wg_d[fg], wu_d[fg])
        for fc in range(2):
            fchunk = fg * 2 + fc
            for sub in range(2):
                gb = C.i_g % 2
                C.i_g += 1
                s0 = sub * SUB
                for dk in range(DK):
                    P.op("pe", lambda e, dk=dk, gb=gb, slot=slot, fc=fc, s0=s0: e.matmul(
                        C.G[gb][:], lhsT=C.wgs[slot][:, dk, fc * 128:(fc + 1) * 128],
                        rhs=C.xn[:, dk, s0:s0 + SUB], start=(dk == 0), stop=(dk == DK - 1)),
                        reads=[("wgs", slot), ("xn", sub)], writes=[("G", gb)])
                for dk in range(DK):
                    P.op("pe", lambda e, dk=dk, gb=gb, slot=slot, fc=fc, s0=s0: e.matmul(
                        C.U[gb][:], lhsT=C.wus[slot][:, dk, fc * 128:(fc + 1) * 128],
                        rhs=C.xn[:, dk, s0:s0 + SUB], start=(dk == 0), stop=(dk == DK - 1)),
                        reads=[("wus", slot), ("xn", sub)], writes=[("U", gb)])
                P.op("act", lambda e, gb=gb: e.activation(out=C.st[gb][:], in_=C.G[gb][:], func=AF.Silu),
                     reads=[("G", gb)], writes=[("st", gb)])
                P.op("dve", lambda e, gb=gb, fchunk=fchunk, s0=s0: e.tensor_tensor(
                    out=C.act[:, fchunk, s0:s0 + SUB], in0=C.st[gb][:], in1=C.U[gb][:], op=ALU.mult),
                    reads=[("st", gb), ("U", gb)], writes=[("act", sub)])
    for dc in range(DK):
        slot = C.i_wd % 2
        C.i_wd += 1
        P.dma("pool", C.wds[slot][:].rearrange("p a b -> p (a b)"), wd_d[dc],
              reads=[], writes=[("wds", slot)], key=("wds", slot))
        for sub in range(2):
            db = C.i_d % 2
            C.i_d += 1
            s0 = sub * SUB
            for fk in range(FK):
                P.op("pe", lambda e, fk=fk, db=db, slot=slot, s0=s0: e.matmul(
                    C.Dn[db][:], lhsT=C.wds[slot][:, fk, :], rhs=C.act[:, fk, s0:s0 + SUB],
                    start=(fk == 0), stop=(fk == FK - 1)),
                    reads=[("wds", slot), ("act", sub)], writes=[("Dn", db)])
            P.op("dve", lambda e, db=db, dc=dc, s0=s0: e.scalar_tensor_tensor(
                out=C.hT[:, dc, s0:s0 + SUB], in0=C.Dn[db][:], scalar=0.5,
                in1=C.hT[:, dc, s0:s0 + SUB], op0=ALU.mult, op1=ALU.add),
                reads=[("Dn", db)], writes=[("hT", sub)])


def emit_inproj(P, C, win_d, gain_sb, gtok, zT_out, v_out, t0):
    for sub in range(2):
        emit_norm(P, C, sub, gain_sb, gtok)
    for mg in range(6):
        slot = load_wgu(P, C, win_d[mg], None)
        for mc2 in range(2):
            mc = mg * 2 + mc2
            for sub in range(2):
                gb = C.i_g % 2
                C.i_g += 1
                zb = C.i_z % 2
                C.i_z += 1
                s0 = sub * SUB
                for dk in range(DK):
                    P.op("pe", lambda e, dk=dk, gb=gb, slot=slot, mc2=mc2, s0=s0: e.matmul(
                        C.G[gb][:], lhsT=C.wgs[slot][:, dk, mc2 * 128:(mc2 + 1) * 128],
                        rhs=C.xn[:, dk, s0:s0 + SUB], start=(dk == 0), stop=(dk == DK - 1)),
                        reads=[("wgs", slot), ("xn", sub)], writes=[("G", gb)])
                P.op("act", lambda e, gb=gb, zb=zb: e.copy(out=C.zst[zb][:], in_=C.G[gb][:]),
                     reads=[("G", gb)], writes=[("zst", zb)])
                P.dma("sp", zT_out[mc, :, t0 + s0:t0 + s0 + SUB], C.zst[zb][:],
                      reads=[("zst", zb)], writes=[("zT_out",)], key=("zst", zb))
    for mg in range(6, 8):
        slot = load_wgu(P, C, win_d[mg], None)
        for tb in range(NT // 128):
            sub = tb // 4
            gb = C.i_g % 2
            C.i_g += 1
            zb = C.i_z % 2
            C.i_z += 1
            for dk in range(DK):
                P.op("pe", lambda e, dk=dk, gb=gb, slot=slot, tb=tb: e.matmul(
                    C.U[gb][:, 0:256], lhsT=C.xn[:, dk, tb * 128:(tb + 1) * 128],
                    rhs=C.wgs[slot][:, dk, :], start=(dk == 0), stop=(dk == DK - 1)),
                    reads=[("wgs", slot), ("xn", sub)], writes=[("U", gb)])
            P.op("act", lambda e, gb=gb, zb=zb: e.copy(out=C.zst[zb][:, 0:256], in_=C.U[gb][:, 0:256]),
                 reads=[("U", gb)], writes=[("zst", zb)])
            P.dma("sp", v_out[t0 + tb * 128:t0 + (tb + 1) * 128, (mg - 6) * 256:(mg - 5) * 256],
                  C.zst[zb][:, 0:256], reads=[("zst", zb)], writes=[("v_out",)], key=("zst", zb))


def load_small(P, C, name, src, shape, dtype=F32, queue="sp"):
    t = P.sb(name, shape, dtype)
    P.dma(queue, t[:], src, reads=[], writes=[(name,)], key=(name,))
    return t


def build_A():
    P = Prog()
    nc = P.nc
    C = Ctx()
    hT_in = P.dram("hT_in", [128, DK, LH], F32, kind="ExternalInput")
    g_ffn1 = P.dram("g_ffn1", [128, DK], F32, kind="ExternalInput")
    g_mix = P.dram("g_mix", [128, DK], F32, kind="ExternalInput")
    wg_d = P.dram("wg", [11, 128, 2048], F32, kind="ExternalInput")
    wu_d = P.dram("wu", [11, 128, 2048], F32, kind="ExternalInput")
    wd_d = P.dram("wd", [8, 128, FK * 128], F32, kind="ExternalInput")
    win_d = P.dram("win", [8, 128, 2048], F32, kind="ExternalInput")
    hT_out = P.dram("hT_out", [128, DK, LH], F32, kind="ExternalOutput")
    zT_out = P.dram("zT_out", [12, 128, LH], BF16, kind="ExternalOutput")
    v_out = P.dram("v_out", [LH, 512], BF16, kind="ExternalOutput")
    owner_common(P, C)
    C.epsb = P.sb("epsb", [128, 1], F32)
    P.op("dve", lambda e: e.memset(C.epsb[:], EPS), writes=[("epsb",)])
    gf = load_small(P, C, "gf", g_ffn1[:, :], [128, DK])
    gm = load_small(P, C, "gm", g_mix[:, :], [128, DK])
    for ti in range(LH // NT):
        t0 = ti * NT
        P.dma("sp", C.hT[:], hT_in[:, :, t0:t0 + NT], reads=[],
              writes=[("hT", 0), ("hT", 1)], key=("hT",))
        emit_ffn(P, C, wg_d, wu_d, wd_d, gf, ("gf",))
        P.dma("sp", hT_out[:, :, t0:t0 + NT], C.hT[:], reads=[("hT", 0), ("hT", 1)],
              writes=[("hT_out",)], key=("hTst",))
        emit_inproj(P, C, win_d, gm, ("gm",), zT_out, v_out, t0)
    P.emit()
    return nc


def lay_gain(g):
    return np.ascontiguousarray(g.reshape(DK, 128).T)


def lay_w_kgroups(w, gw=256):
    n = w.shape[1]
    kc = w.shape[0] // 128
    a = w.reshape(kc, 128, n // gw, gw).transpose(2, 1, 0, 3)
    return np.ascontiguousarray(a).reshape(n // gw, 128, kc * gw)


def lay_wd(w):
    a = w.reshape(FK, 128, DK, 128).transpose(2, 1, 0, 3)
    return np.ascontiguousarray(a).reshape(DK, 128, FK * 128)


def lay_hT(xb):
    return np.ascontiguousarray(xb.T.reshape(DK, 128, xb.shape[0]).transpose(1, 0, 2))


def unlay_hT(hT):
    return np.ascontiguousarray(hT.transpose(1, 0, 2).reshape(D, hT.shape[2]).T)


class Arena:
    def __init__(self, P, name, nbytes):
        self.t = P.sb(name, [128, nbytes // 4], F32)
        self.f = self.t[:]
        self.h = self.t[:].bitcast(BF16)
        self.off = 0
        self.cap = nbytes

    def alloc(self, shape, dtype, at=None):
        n = 1
        for s in shape:
            n *= s
        esz = 4 if dtype == F32 else 2
        if at is None:
            base = self.off
            self.off += (n * esz + 3) // 4 * 4
            assert self.off <= self.cap, (self.off, self.cap)
        else:
            base = at
        if dtype == F32:
            ap = self.f[:, base // 4: base // 4 + n]
        else:
            ap = self.h[:, base // 2: base // 2 + n]
        if len(shape) == 1:
            return ap
        names = " ".join("d%d" % i for i in range(len(shape)))
        kw = {"d%d" % i: shape[i] for i in range(1, len(shape))}
        return ap.rearrange("p (%s) -> p %s" % (names, names), **kw)


def bins(ap, pos, count):
    pairs = [list(x) for x in ap.ap]
    pairs.insert(pos, [0, count])
    return bass.AP(ap.tensor, ap.offset, pairs)


TWO_PI = 2.0 * math.pi


GELU_C = 2.0 * math.sqrt(2.0 / math.pi)


def emit_gelu(P, out, x, tmp, r, w, ttok):
    P.op("dve", lambda e: e.tensor_tensor(out=tmp, in0=x, in1=x, op=ALU.mult), reads=r, writes=[ttok])
    P.op("dve", lambda e: e.tensor_scalar(out=tmp, in0=tmp, scalar1=0.044715, scalar2=1.0,
                                          op0=ALU.mult, op1=ALU.add), reads=[ttok], writes=[ttok])
    P.op("dve", lambda e: e.tensor_tensor(out=tmp, in0=tmp, in1=x, op=ALU.mult), reads=[ttok] + r, writes=[ttok])
    P.op("act", lambda e: e.activation(out=tmp, in_=tmp, func=AF.Sigmoid, scale=GELU_C),
         reads=[ttok], writes=[ttok])
    P.op("dve", lambda e: e.tensor_tensor(out=out, in0=tmp, in1=x, op=ALU.mult), reads=[ttok] + r, writes=w)


def emit_ssm(P, C, A, I, gy_out):
    PS = C.PS
    dve = "dve"

    def tt(out, in0, in1, op, r, w, eng=dve):
        P.op(eng, lambda e: e.tensor_tensor(out=out, in0=in0, in1=in1, op=op), reads=r, writes=w)

    def ts(out, in0, s1, s2, op0, op1, r, w, eng=dve):
        if op1 is None:
            P.op(eng, lambda e: e.tensor_scalar(out=out, in0=in0, scalar1=s1, scalar2=None, op0=op0),
                 reads=r, writes=w)
        else:
            P.op(eng, lambda e: e.tensor_scalar(out=out, in0=in0, scalar1=s1, scalar2=s2, op0=op0, op1=op1),
                 reads=r, writes=w)

    def actf(out, in_, func, r, w, bias=None, scale=1.0):
        if bias is None:
            P.op("act", lambda e: e.activation(out=out, in_=in_, func=func, scale=scale), reads=r, writes=w)
        else:
            P.op("act", lambda e: e.activation(out=out, in_=in_, func=func, bias=bias, scale=scale),
                 reads=r, writes=w)

    def ld(shape, src, name, dtype=F32):
        t = A.alloc(shape, dtype)
        P.dma("sp", t, src, reads=[], writes=[(name,)], key=(name,))
        return t

    uT = A.alloc([2, L], BF16)
    P.dma("sp", uT[:, 0, :], I["uT"][0, :, :], reads=[], writes=[("uT",)], key=("uT",))
    P.dma("sp", uT[:, 1, :], I["uT"][1, :, :], reads=[], writes=[("uT",)], key=("uT",))
    Bw = A.alloc([2, 32, 2, 128], BF16)
    CL = A.alloc([8, 33, 2, 32], BF16)
    Kt = A.alloc([2, 32, 128], BF16)
    bbf = A.alloc([8, 2, 32], BF16)
    wst_off = A.off
    S_sb = A.alloc([256, 2, 8], F32)
    XhF = A.alloc([257, 2, 8], F32)
    Xh = A.alloc([8, 2, 256], BF16)
    AA1 = A.alloc([2, 8], F32)
    AA2n = A.alloc([8], F32)
    AA2p = A.alloc([8], F32)
    d_sp = ld([2], I["d_sp"][:, :], "d_sp")
    ysb = [A.alloc([SUB], F32) for _ in range(2)]
    gyb = [A.alloc([SUB], BF16) for _ in range(2)]
    gtmp = [A.alloc([SUB], F32) for _ in range(2)]
    mark = A.off
    lam_re = ld([8], I["lam_re_sp"][:, :], "lam_re")
    lam_im = ld([8], I["lam_im_sp"][:, :], "lam_im")
    logdt = ld([8], I["logdt_sp"][:, :], "logdt")
    jr = ld([33], I["jramp"][:, :], "jramp")
    ident = ld([128], I["ident"][:, :], "ident")
    bp_re = ld([8, 32], I["bpad_re"][:, :, :], "bp_re")
    bp_im = ld([8, 32], I["bpad_im"][:, :, :], "bp_im")
    cp_re = ld([8, 32], I["cpad_re"][:, :, :], "cp_re")
    cp_im = ld([8, 32], I["cpad_im"][:, :, :], "cp_im")
    lre = A.alloc([8], F32)
    dt = A.alloc([8], F32)
    aa = A.alloc([8], F32)
    th = A.alloc([8], F32)
    T = "prep"
    ts(lre, lam_re, -1e-4, None, ALU.min, None, [("lam_re",)], [T])
    actf(dt, logdt, AF.Exp, [("logdt",)], [T])
    tt(aa, lre, dt, ALU.mult, [T], [T])
    tt(th, lam_im, dt, ALU.mult, [T, ("lam_im",)], [T])
    I32 = mybir.dt.int32

    def rr(out, x, shift, shape):
        xs = A.alloc(shape, F32)
        kf = A.alloc(shape, F32)
        ki = A.alloc(shape, F32).bitcast(I32)
        ts(xs, x, float(shift), None, ALU.add, None, [T], [T])
        ts(kf, xs, 1.0 / TWO_PI, None, ALU.mult, None, [T], [T])
        P.op(dve, lambda e: e.tensor_copy(out=ki, in_=kf), reads=[T], writes=[T])
        P.op(dve, lambda e: e.tensor_copy(out=kf, in_=ki), reads=[T], writes=[T])
        P.op(dve, lambda e: e.scalar_tensor_tensor(out=xs, in0=kf, scalar=-TWO_PI, in1=xs,
                                                   op0=ALU.mult, op1=ALU.add), reads=[T], writes=[T])
        ts(kf, xs, math.pi, None, ALU.is_gt, None, [T], [T])
        P.op(dve, lambda e: e.scalar_tensor_tensor(out=xs, in0=kf, scalar=-TWO_PI, in1=xs,
                                                   op0=ALU.mult, op1=ALU.add), reads=[T], writes=[T])
        ts(kf, xs, -math.pi, None, ALU.is_lt, None, [T], [T])
        P.op(dve, lambda e: e.scalar_tensor_tensor(out=xs, in0=kf, scalar=TWO_PI, in1=xs,
                                                   op0=ALU.mult, op1=ALU.add), reads=[T], writes=[T])
        ts(out, xs, -3.141592, 3.141592, ALU.max, ALU.min, [T], [T])

    rr(th, th, 0.0, [8])
    arg = A.alloc([8, 33], F32)
    Lmag = A.alloc([8, 33], F32)
    ang = A.alloc([8, 33], F32)
    sinv = A.alloc([8, 33], F32)
    cosv = A.alloc([8, 33], F32)
    Lre = A.alloc([8, 33], F32)
    Lim = A.alloc([8, 33], F32)
    jr_b = bins(jr, 1, 8)
    tt(arg, bins(aa, 2, 33), jr_b, ALU.mult, [T, ("jramp",)], [T])
    actf(Lmag, arg, AF.Exp, [T], [T])
    tt(ang, bins(th, 2, 33), jr_b, ALU.mult, [T], [T])
    rr(arg, ang, 0.0, [8, 33])
    actf(sinv, arg, AF.Sin, [T], [T])
    rr(arg, ang, 0.5 * math.pi, [8, 33])
    actf(cosv, arg, AF.Sin, [T], [T])
    tt(Lre, Lmag, cosv, ALU.mult, [T], [T])
    tt(Lim, Lmag, sinv, ALU.mult, [T], [T])
    nr = A.alloc([8], F32)
    den = A.alloc([8], F32)
    t8a = A.alloc([8], F32)
    t8b = A.alloc([8], F32)
    cr = A.alloc([8], F32)
    ci = A.alloc([8], F32)
    L1re = Lre[:, :, 1]
    L1im = Lim[:, :, 1]
    ts(nr, L1re, -1.0, None, ALU.add, None, [T], [T])
    tt(den, lre, lre, ALU.mult, [T], [T])
    tt(t8a, lam_im, lam_im, ALU.mult, [T], [T])
    tt(den, den, t8a, ALU.add, [T], [T])
    P.op(dve, lambda e: e.reciprocal(out=den, in_=den), reads=[T], writes=[T])
    tt(t8a, nr, lre, ALU.mult, [T], [T])
    tt(t8b, L1im, lam_im, ALU.mult, [T], [T])
    tt(t8a, t8a, t8b, ALU.add, [T], [T])
    tt(cr, t8a, den, ALU.mult, [T], [T])
    tt(t8a, L1im, lre, ALU.mult, [T], [T])
    tt(t8b, nr, lam_im, ALU.mult, [T], [T])
    tt(t8a, t8a, t8b, ALU.subtract, [T], [T])
    tt(ci, t8a, den, ALU.mult, [T], [T])
    bb_re = A.alloc([8, 32], F32)
    bb_im = A.alloc([8, 32], F32)
    t256a = A.alloc([8, 32], F32)
    t256b = A.alloc([8, 32], F32)
    cr_b = bins(cr, 2, 32)
    ci_b = bins(ci, 2, 32)
    tt(t256a, cr_b, bp_re, ALU.mult, [T, ("bp_re",)], [T])
    tt(t256b, ci_b, bp_im, ALU.mult, [T, ("bp_im",)], [T])
    tt(bb_re, t256a, t256b, ALU.subtract, [T], [T])
    tt(t256a, cr_b, bp_im, ALU.mult, [T], [T])
    tt(t256b, ci_b, bp_re, ALU.mult, [T], [T])
    tt(bb_im, t256a, t256b, ALU.add, [T], [T])
    P.op(dve, lambda e: e.tensor_copy(out=bbf[:, :, 0, :], in_=bb_re), reads=[T], writes=[T])
    P.op(dve, lambda e: e.tensor_copy(out=bbf[:, :, 1, :], in_=bb_im), reads=[T], writes=[T])
    P.op(dve, lambda e: e.tensor_copy(out=AA1[:, 0, :], in_=Lre[:, :, 32]), reads=[T], writes=[T])
    P.op(dve, lambda e: e.tensor_copy(out=AA1[:, 1, :], in_=Lre[:, :, 32]), reads=[T], writes=[T])
    P.op(dve, lambda e: e.tensor_copy(out=AA2p, in_=Lim[:, :, 32]), reads=[T], writes=[T])
    ts(AA2n, Lim[:, :, 32], -1.0, None, ALU.mult, None, [T], [T])
    Wst = A.alloc([2, 32, 4, 32], F32, at=wst_off)
    tA = A.alloc([33, 32], F32)
    tB = A.alloc([33, 32], F32)
    ntr = 0
    for mb in range(2):
        for m4 in range(4):
            m = 4 * mb + m4
            lre_b = bins(Lre[:, m, 0:32], 2, 32)
            lim_b = bins(Lim[:, m, 0:32], 2, 32)
            bre_b = bins(bb_re[:, m, :], 1, 32)
            bim_b = bins(bb_im[:, m, :], 1, 32)
            tt(tA[:, 0:32, :], lre_b, bre_b, ALU.mult, [T], [T])
            tt(tB[:, 0:32, :], lim_b, bim_b, ALU.mult, [T], [T])
            tt(Wst[:, 0, :, m4, :], tA[:, 0:32, :], tB[:, 0:32, :], ALU.subtract, [T, ("Wst_r",)], [T, ("Wst",)])
            tt(tA[:, 0:32, :], lre_b, bim_b, ALU.mult, [T], [T])
            tt(tB[:, 0:32, :], lim_b, bre_b, ALU.mult, [T], [T])
            tt(Wst[:, 1, :, m4, :], tA[:, 0:32, :], tB[:, 0:32, :], ALU.add, [T, ("Wst_r",)], [T, ("Wst",)])
        for j2 in range(16):
            bank = PS[ntr % 4]
            btok = ("PS", ntr % 4)
            ntr += 1
            for q in range(4):
                j = j2 * 2 + q // 2
                part = q % 2
                P.op("pe", lambda e, bank=bank, q=q, j=j, part=part: e.transpose(
                    out=bank[:, q * 128:(q + 1) * 128],
                    in_=Wst[:, part, j, :, :].rearrange("p a b -> p (a b)"), identity=ident),
                    reads=[("Wst",), ("ident",)], writes=[btok])
            P.op("act", lambda e, bank=bank, mb=mb, j2=j2: e.copy(
                out=Bw[:, mb, 2 * j2:2 * j2 + 2, :, :].rearrange("p a b c -> p (a b c)"), in_=bank[:, :]),
                reads=[btok], writes=[("Bw",)])
    for m in range(8):
        lre_b = bins(Lre[:, m, :], 2, 32)
        lim_b = bins(Lim[:, m, :], 2, 32)
        cre_b = bins(cp_re[:, m, :], 1, 33)
        cim_b = bins(cp_im[:, m, :], 1, 33)
        tt(tA, lre_b, cre_b, ALU.mult, [T, ("cp_re",)], [T])
        tt(tB, lim_b, cim_b, ALU.mult, [T, ("cp_im",)], [T])
        tt(CL[:, m, :, 0, :], tA, tB, ALU.subtract, [T], [T, ("CL",)])
        tt(tA, lre_b, cim_b, ALU.mult, [T], [T])
        tt(tB, lim_b, cre_b, ALU.mult, [T], [T])
        P.op(dve, lambda e, m=m: e.scalar_tensor_tensor(
            out=CL[:, m, :, 1, :], in0=tA, scalar=-1.0, in1=tB, op0=ALU.mult, op1=ALU.subtract),
            reads=[T], writes=[T, ("CL",)])
    P.op("pool", lambda e: e.memset(Kt, 0.0), writes=[("Kt",)])
    for m in range(8):
        mb, m4 = m // 4, m % 4
        for jh in range(2):
            bank = PS[4 + (m * 2 + jh) % 4]
            btok = ("PS", 4 + (m * 2 + jh) % 4)
            for part in range(2):
                P.op("pe", lambda e, bank=bank, m=m, m4=m4, jh=jh, part=part: e.matmul(
                    bank[32 * m4:32 * m4 + 32, :], lhsT=bbf[:, m, part, :],
                    rhs=CL[:, m, jh * 16:(jh + 1) * 16, part, :],
                    start=(part == 0), stop=(part == 1), tile_position=(0, 32 * m4)),
                    reads=[T, ("CL",)], writes=[btok])
            P.op(dve, lambda e, bank=bank, mb=mb, m4=m4, jh=jh: e.tensor_copy(
                out=Kt[32 * m4:32 * m4 + 32, mb, jh * 16:(jh + 1) * 16, 32 * m4:32 * m4 + 32],
                in_=bank[32 * m4:32 * m4 + 32, :].rearrange("p (a b) -> p a b", b=32)),
                reads=[btok], writes=[("Kt",)])
    nev = 0
    for mb in range(2):
        for j in range(32):
            for m4 in range(4):
                for part in range(2):
                    bi = m4 * 2 + part
                    P.op("pe", lambda e, bi=bi, mb=mb, j=j, m4=m4, part=part: e.matmul(
                        PS[bi][:, 0:256], lhsT=Bw[32 * m4:32 * m4 + 32, mb, j, part, :],
                        rhs=uT[32 * m4:32 * m4 + 32, mb, (31 - j) * 256:(32 - j) * 256],
                        start=(j == 0), stop=(j == 31), tile_position=(32 * m4, 0)),
                        reads=[("Bw",), ("uT",)], writes=[("PS", bi)])
        for m4 in range(4):
            for part in range(2):
                bi = m4 * 2 + part
                eng = "act" if nev % 2 == 0 else dve
                nev += 1
                if eng == "act":
                    P.op("act", lambda e, bi=bi, mb=mb, m4=m4, part=part: e.copy(
                        out=S_sb[:, :, part, 4 * mb + m4], in_=PS[bi][:, 0:256]),
                        reads=[("PS", bi)], writes=[("S_sb", bi % 2)])
                else:
                    P.op(dve, lambda e, bi=bi, mb=mb, m4=m4, part=part: e.tensor_copy(
                        out=S_sb[:, :, part, 4 * mb + m4], in_=PS[bi][:, 0:256]),
                        reads=[("PS", bi)], writes=[("S_sb", bi % 2)])
    Ut = A.alloc([2, 8], F32)
    Vt = A.alloc([2, 8], F32)
    X = "scan"
    P.op(dve, lambda e: e.memset(XhF[:, 0, :, :], 0.0), reads=[("S_sb", 0), ("S_sb", 1), T], writes=[X])
    for c in range(256):
        Xp = XhF[:, c, :, :]
        tt(Ut, Xp, AA1, ALU.mult, [X], [X])
        tt(Vt[:, 0, :], XhF[:, c, 1, :], AA2n, ALU.mult, [X], [X])
        tt(Vt[:, 1, :], XhF[:, c, 0, :], AA2p, ALU.mult, [X], [X])
        tt(Ut, Ut, Vt, ALU.add, [X], [X])
        tt(XhF[:, c + 1, :, :], Ut, S_sb[:, c, :, :], ALU.add, [X], [X])
    P.op(dve, lambda e: e.tensor_copy(out=Xh, in_=XhF[:, 0:256, :, :].rearrange("p c a m -> p m a c")),
         reads=[X], writes=[("Xh",)])
    it = 0
    for mb in range(2):
        for tp in range(16):
            bi = it % 2
            it += 1
            bank = PS[bi]
            for tl in range(2):
                tau = 2 * tp + tl
                cols = slice(tl * 256, (tl + 1) * 256)
                for m4 in range(4):
                    for part in range(2):
                        P.op("pe", lambda e, bank=bank, cols=cols, m4=m4, part=part, mb=mb, tau=tau: e.matmul(
                            bank[32 * m4:32 * m4 + 32, cols], lhsT=CL[:, 4 * mb + m4, tau + 1, part, :],
                            rhs=Xh[:, 4 * mb + m4, part, :], start=(part == 0), stop=False,
                            tile_position=(0, 32 * m4)),
                            reads=[("CL",), ("Xh",)], writes=[("PS", bi)])
                for j in range(tau + 1):
                    P.op("pe", lambda e, bank=bank, cols=cols, mb=mb, j=j, tau=tau: e.matmul(
                        bank[:, cols], lhsT=Kt[:, mb, j, :],
                        rhs=uT[:, mb, (tau - j) * 256:(tau - j + 1) * 256],
                        start=False, stop=(j == tau)),
                        reads=[("Kt",), ("uT",)], writes=[("PS", bi)])
            c0 = 2 * tp * 256
            P.op(dve, lambda e, bank=bank, bi=bi, mb=mb, c0=c0: e.scalar_tensor_tensor(
                out=ysb[bi], in0=uT[:, mb, c0:c0 + 512], scalar=d_sp[:, mb:mb + 1], in1=bank[:, :],
                op0=ALU.mult, op1=ALU.add),
                reads=[("PS", bi), ("uT",), ("d_sp",)], writes=[("ysb", bi)])
            emit_gelu(P, gyb[bi], ysb[bi], gtmp[bi], [("ysb", bi)], [("gyb", bi)], ("gtmp", bi))
            P.dma("sp", gy_out[mb, :, c0:c0 + 512], gyb[bi], reads=[("gyb", bi)],
                  writes=[("gy_out",)], key=("gyb", bi))
    return mark


def emit_attn(P, C, A, I, oT_out):
    PS = C.PS
    dve = "dve"
    ones = A.alloc([128], BF16)
    P.op(dve, lambda e: e.memset(ones, 1.0), writes=[("ones",)])
    epsb = A.alloc([1], F32)
    P.op(dve, lambda e: e.memset(epsb, EPS), writes=[("epsb",)])

    def ld(shape, src, name, dtype=F32):
        t = A.alloc(shape, dtype)
        P.dma("sp", t, src, reads=[], writes=[(name,)], key=(name,))
        return t

    lq1 = ld([64], I["lq1"][:, :], "lq1")
    lk1 = ld([64], I["lk1"][:, :], "lk1")
    lq2 = ld([64], I["lq2"][:, :], "lq2")
    lk2 = ld([64], I["lk2"][:, :], "lk2")
    subln = ld([1], I["subln_sp"][:, :], "subln")
    b31 = ld([2], I["b31"][:, :], "b31")
    laminit = ld([1], I["laminit"][:, :], "laminit")
    mask = ld([5, SUB], I["mask"][:, :, :], "mask")
    BM = [ld([5, SUB], I["bias_g"][hh, :, :, :], "bias_g%d" % hh) for hh in range(2)]
    for hh in range(2):
        P.op(dve, lambda e, hh=hh: e.tensor_tensor(out=BM[hh], in0=BM[hh], in1=mask, op=ALU.add),
             reads=[("mask",), ("bias_g%d" % hh,)], writes=[("BM", hh)])
    s1 = A.alloc([1], F32)
    s2 = A.alloc([1], F32)
    pr = A.alloc([64], F32)
    neglam = A.alloc([1], F32)
    gsub = A.alloc([1], F32)
    T = "lamprep"
    P.op(dve, lambda e: e.tensor_tensor(out=pr, in0=lq1, in1=lk1, op=ALU.mult), reads=[("lq1",), ("lk1",)], writes=[T])
    P.op(dve, lambda e: e.reduce_sum(out=s1, in_=pr, axis=AX.X), reads=[T], writes=[T])
    P.op(dve, lambda e: e.tensor_tensor(out=pr, in0=lq2, in1=lk2, op=ALU.mult), reads=[("lq2",), ("lk2",), T], writes=[T])
    P.op(dve, lambda e: e.reduce_sum(out=s2, in_=pr, axis=AX.X), reads=[T], writes=[T])
    P.op("act", lambda e: e.activation(out=s1, in_=s1, func=AF.Exp), reads=[T], writes=[T])
    P.op("act", lambda e: e.activation(out=s2, in_=s2, func=AF.Exp), reads=[T], writes=[T])
    P.op(dve, lambda e: e.tensor_tensor(out=neglam, in0=s2, in1=s1, op=ALU.subtract), reads=[T], writes=[T])
    P.op(dve, lambda e: e.tensor_tensor(out=neglam, in0=neglam, in1=laminit, op=ALU.subtract),
         reads=[T, ("laminit",)], writes=[T])
    P.op(dve, lambda e: e.tensor_scalar(out=gsub, in0=laminit, scalar1=-1.0, scalar2=1.0,
                                        op0=ALU.mult, op1=ALU.add), reads=[T], writes=[T])
    P.op(dve, lambda e: e.tensor_tensor(out=gsub, in0=gsub, in1=subln, op=ALU.mult),
         reads=[T, ("subln",)], writes=[T])
    qT = A.alloc([L], BF16)
    kT = A.alloc([L], BF16)
    V = A.alloc([64, 128], BF16)
    Pb = [[A.alloc([SUB], BF16) for _ in range(2)] for _ in range(2)]
    Tb = [[A.alloc([SUB], F32) for _ in range(2)] for _ in range(2)]
    r1 = A.alloc([SUB], F32)
    r2 = A.alloc([SUB], F32)
    o1 = A.alloc([SUB], F32)
    o2 = A.alloc([SUB], F32)
    sqb = A.alloc([SUB], BF16)
    rsb = A.alloc([SUB], F32)
    outb = [A.alloc([SUB], BF16) for _ in range(2)]
    O = [PS[4], PS[5]]
    Lb = [PS[6], PS[7]]
    it = 0
    no = 0
    for hh in range(2):
        P.dma("sp", qT, I["qT"][hh, :, :], reads=[], writes=[("qT",)], key=("qT",))
        P.dma("sp", kT, I["kT"][hh, :, :], reads=[], writes=[("kT",)], key=("kT",))
        P.dma("sp", V, I["V"][hh, :, :, :], reads=[], writes=[("V",)], key=("V",))
        for g in range(16):
            nkb = 4 * g + 4
            q0 = g * SUB
            for kb in range(nkb):
                sl = it % 2
                it += 1
                for mp in range(2):
                    bi = sl * 2 + mp
                    P.op("pe", lambda e, bi=bi, mp=mp, kb=kb, q0=q0: e.matmul(
                        PS[bi][:, :], lhsT=kT[64 * mp:64 * mp + 64, kb * 128:(kb + 1) * 128],
                        rhs=qT[64 * mp:64 * mp + 64, q0:q0 + SUB], start=True, stop=True,
                        tile_position=(64 * mp, 0)),
                        reads=[("qT",), ("kT",)], writes=[("PS", bi)])
                near = kb >= 4 * g - 1
                for mp in range(2):
                    bi = sl * 2 + mp
                    if near:
                        r = kb - 4 * g + 1
                        P.op(dve, lambda e, bi=bi, sl=sl, mp=mp, hh=hh, r=r: e.scalar_tensor_tensor(
                            out=Tb[sl][mp], in0=PS[bi][:, :], scalar=0.125, in1=BM[hh][:, r, :],
                            op0=ALU.mult, op1=ALU.add),
                            reads=[("PS", bi), ("BM", hh)], writes=[("Tb", sl, mp)])
                        P.op("act", lambda e, sl=sl, mp=mp: e.activation(
                            out=Pb[sl][mp], in_=Tb[sl][mp], func=AF.Exp),
                            reads=[("Tb", sl, mp)], writes=[("Pb", sl, mp)])
                    else:
                        P.op("act", lambda e, bi=bi, sl=sl, mp=mp, hh=hh: e.activation(
                            out=Pb[sl][mp], in_=PS[bi][:, :], func=AF.Exp,
                            bias=b31[:, hh:hh + 1], scale=0.125),
                            reads=[("PS", bi), ("b31",)], writes=[("Pb", sl, mp)])
                for mp in range(2):
                    P.op("pe", lambda e, sl=sl, mp=mp, kb=kb, nkb=nkb: e.matmul(
                        O[mp][:, :], lhsT=V[:, kb, :], rhs=Pb[sl][mp],
                        start=(kb == 0), stop=(kb == nkb - 1)),
                        reads=[("V",), ("Pb", sl, mp)], writes=[("O", mp)])
                    P.op("pe", lambda e, sl=sl, mp=mp, kb=kb, nkb=nkb: e.matmul(
                        Lb[mp][:, :], lhsT=ones, rhs=Pb[sl][mp],
                        start=(kb == 0), stop=(kb == nkb - 1)),
                        reads=[("ones",), ("Pb", sl, mp)], writes=[("Lb", mp)])
            P.op(dve, lambda e: e.reciprocal(out=r1, in_=Lb[0][:, :]), reads=[("Lb", 0)], writes=[("r1",)])
            P.op(dve, lambda e: e.reciprocal(out=r2, in_=Lb[1][:, :]), reads=[("Lb", 1)], writes=[("r2",)])
            P.op(dve, lambda e: e.tensor_tensor(out=o1, in0=O[0][:, :], in1=r1, op=ALU.mult),
                 reads=[("O", 0), ("r1",)], writes=[("o1",)])
            P.op(dve, lambda e: e.tensor_tensor(out=o2, in0=O[1][:, :], in1=r2, op=ALU.mult),
                 reads=[("O", 1), ("r2",)], writes=[("o2",)])
            P.op(dve, lambda e: e.scalar_tensor_tensor(out=o1, in0=o2, scalar=neglam[:, 0:1], in1=o1,
                                                       op0=ALU.mult, op1=ALU.add),
                 reads=[("o1",), ("o2",), T], writes=[("o1",)])
            P.op("act", lambda e: e.activation(out=sqb, in_=o1, func=AF.Square),
                 reads=[("o1",)], writes=[("sqb",)])
            nb = (it % 2) * 2
            P.op("pe", lambda e, nb=nb: e.matmul(PS[nb][:, :], lhsT=ones, rhs=sqb, start=True, stop=True),
                 reads=[("sqb",), ("ones",)], writes=[("PS", nb)])
            P.op("act", lambda e, nb=nb: e.activation(out=rsb, in_=PS[nb][:, :], func=AF.Sqrt,
                                                      bias=epsb[:, 0:1], scale=1.0 / 128),
                 reads=[("PS", nb), ("epsb",)], writes=[("rsb",)])
            P.op(dve, lambda e: e.reciprocal(out=rsb, in_=rsb), reads=[("rsb",)], writes=[("rsb",)])
            ob = no % 2
            no += 1
            P.op(dve, lambda e, ob=ob: e.scalar_tensor_tensor(
                out=outb[ob], in0=o1, scalar=gsub[:, 0:1], in1=rsb, op0=ALU.mult, op1=ALU.mult),
                reads=[("o1",), ("rsb",), T], writes=[("outb", ob)])
            P.dma("sp", oT_out[hh, :, q0:q0 + SUB], outb[ob], reads=[("outb", ob)],
                  writes=[("oT_out",)], key=("outb", ob))


def build_B(do_ssm=True, do_attn=True):
    P = Prog()
    nc = P.nc
    C = Ctx()
    I = {}

    def din(name, shape, dtype=F32):
        I[name] = P.dram(name, shape, dtype, kind="ExternalInput")

    din("uT", [2, 128, L], BF16)
    din("qT", [2, 128, L], BF16)
    din("kT", [2, 128, L], BF16)
    din("V", [2, 128, 64, 128], BF16)
    for n in ("lam_re_sp", "lam_im_sp", "logdt_sp"):
        din(n, [128, 8])
    for n in ("bpad_re", "bpad_im", "cpad_re", "cpad_im"):
        din(n, [128, 8, 32])
    din("d_sp", [128, 2])
    din("jramp", [128, 33])
    din("ident", [128, 128])
    for n in ("lq1", "lk1", "lq2", "lk2"):
        din(n, [128, 64])
    din("subln_sp", [128, 1])
    din("b31", [128, 2])
    din("laminit", [128, 1])
    din("mask", [128, 5, SUB])
    din("bias_g", [2, 128, 5, SUB])
    gy_out = P.dram("gy_out", [2, 128, L], BF16, kind="ExternalOutput")
    oT_out = P.dram("oT_out", [2, 128, L], BF16, kind="ExternalOutput")
    C.PS = [P.ps("ps%d" % i, [128, SUB]) for i in range(8)]
    A = Arena(P, "arena", 200 * 1024)
    if do_ssm:
        emit_ssm(P, C, A, I, gy_out)
    if do_attn:
        if do_ssm:
            P.barrier()
            A.off = 0
        emit_attn(P, C, A, I, oT_out)
    P.emit()
    return nc


def rel_bucket_np(rel):
    n = np.maximum(rel, 0)
    max_exact = 16
    n_f = np.maximum(n, 1).astype(np.float32)
    large = max_exact + (np.log(n_f / np.float32(max_exact)) / np.float32(math.log(128 / max_exact))
                         * np.float32(32 - max_exact)).astype(np.int32)
    large = np.minimum(large, 31)
    return np.where(n < max_exact, n, large)


def attn_consts():
    k = np.arange(128)[:, None, None]
    r = np.arange(5)[None, :, None]
    q = np.arange(512)[None, None, :]
    rel = q - k - 128 * (r - 1)
    mask = np.where(rel >= 0, 0.0, -30000.0).astype(np.float32)
    bucket = rel_bucket_np(rel)
    return mask, bucket


def lay_B_params(inp, l, s):
    m = {}
    gsel = np.arange(16 * s, 16 * s + 16)

    def sp8(a):
        x = a[gsel].reshape(8, 2, 64)
        return np.ascontiguousarray(x.transpose(1, 2, 0).reshape(128, 8))
    m["lam_re_sp"] = sp8(inp["ssm_lambda_re"][l])
    m["lam_im_sp"] = sp8(inp["ssm_lambda_im"][l])
    m["logdt_sp"] = sp8(np.repeat(inp["ssm_log_dt"][l][:, None], 64, axis=1))

    def pad(a_gph):
        x = a_gph[gsel].reshape(8, 2, 64, 16)
        out = np.zeros((2, 64, 8, 2, 16), np.float32)
        for g2 in range(2):
            out[g2, :, :, g2, :] = x[:, g2].transpose(1, 0, 2)
        return np.ascontiguousarray(out.reshape(128, 8, 32))
    m["bpad_re"] = pad(inp["ssm_b_re"][l])
    m["bpad_im"] = pad(inp["ssm_b_im"][l])
    m["cpad_re"] = pad(inp["ssm_c_re"][l].transpose(0, 2, 1))
    m["cpad_im"] = pad(inp["ssm_c_im"][l].transpose(0, 2, 1))
    dd = inp["ssm_d"][l][gsel].reshape(256)
    m["d_sp"] = np.ascontiguousarray(dd.reshape(2, 128).T)
    m["jramp"] = np.ascontiguousarray(np.broadcast_to(np.arange(33, dtype=np.float32), (128, 33)))
    m["ident"] = np.eye(128, dtype=np.float32)
    for a, b in (("lq1", "diff_lambda_q1"), ("lk1", "diff_lambda_k1"), ("lq2", "diff_lambda_q2"),
                 ("lk2", "diff_lambda_k2")):
        m[a] = np.ascontiguousarray(np.broadcast_to(inp[b][l], (128, 64)))
    m["subln_sp"] = np.ascontiguousarray(inp["diff_subln"][l].reshape(128, 1))
    m["b31"] = np.ascontiguousarray(np.broadcast_to(inp["rel_bias"][31, 2 * s:2 * s + 2], (128, 2)))
    lam_init = 0.8 - 0.6 * math.exp(-0.3 * l)
    m["laminit"] = np.full((128, 1), lam_init, np.float32)
    mask, bucket = attn_consts()
    m["mask"] = mask
    m["bias_g"] = np.ascontiguousarray(
        np.stack([inp["rel_bias"][:, 2 * s + hh][bucket] for hh in range(2)]).astype(np.float32))
    return m


def lay_B_acts(zT_pair, v_pair, s):
    z = np.concatenate(zT_pair, axis=2)
    v = np.concatenate(v_pair, axis=0)
    m = {}
    u = z[2 * s:2 * s + 2]
    m["uT"] = np.ascontiguousarray(u.reshape(2, 128, 256, 32).transpose(0, 1, 3, 2).reshape(2, 128, L))
    m["qT"] = np.ascontiguousarray(z[4 + 2 * s:6 + 2 * s])
    m["kT"] = np.ascontiguousarray(z[8 + 2 * s:10 + 2 * s])
    vh = v[:, 256 * s:256 * s + 256].reshape(64, 128, 2, 128)
    m["V"] = np.ascontiguousarray(vh.transpose(2, 1, 0, 3))
    return m


def emit_proj_residual(P, C, w_d, src, stok_fn):
    for mg in range(4):
        slot = load_wgu(P, C, w_d[mg], None)
        for mc2 in range(2):
            dc = mg * 2 + mc2
            for sub in range(2):
                db = C.i_d % 2
                C.i_d += 1
                s0 = sub * SUB
                for kc in range(DK):
                    P.op("pe", lambda e, kc=kc, db=db, slot=slot, mc2=mc2, s0=s0: e.matmul(
                        C.Dn[db][:], lhsT=C.wgs[slot][:, kc, mc2 * 128:(mc2 + 1) * 128],
                        rhs=src[:, kc, s0:s0 + SUB], start=(kc == 0), stop=(kc == DK - 1)),
                        reads=[("wgs", slot)] + stok_fn(sub), writes=[("Dn", db)])
                P.op("dve", lambda e, db=db, dc=dc, s0=s0: e.tensor_tensor(
                    out=C.hT[:, dc, s0:s0 + SUB], in0=C.Dn[db][:], in1=C.hT[:, dc, s0:s0 + SUB], op=ALU.add),
                    reads=[("Dn", db)], writes=[("hT", sub)])


def emit_mixer_out(P, C, I, t0):
    mixT = C.act[:, 0:8, :]
    gyT = C.act[:, 8:12, :]
    P.dma("sp", gyT, I["gyT"][:, :, t0:t0 + NT].rearrange("c p t -> p c t"), reads=[],
          writes=[("gyT",), ("act", 0), ("act", 1)], key=("gyT",))
    P.dma("sp", mixT[:, 4:8, :], I["oT"][:, :, t0:t0 + NT].rearrange("c p t -> p c t"), reads=[],
          writes=[("mixo",), ("act", 0), ("act", 1)], key=("mixo",))
    for vg in range(2):
        slot = load_wgu(P, C, I["wglu"][vg], I["wglu"][2 + vg], nk=4)
        for vc2 in range(2):
            vc = vg * 2 + vc2
            for sub in range(2):
                gb = C.i_g % 2
                C.i_g += 1
                s0 = sub * SUB
                for kc in range(4):
                    P.op("pe", lambda e, kc=kc, gb=gb, slot=slot, vc2=vc2, s0=s0: e.matmul(
                        C.G[gb][:], lhsT=C.wgs[slot][:, kc, vc2 * 128:(vc2 + 1) * 128],
                        rhs=gyT[:, kc, s0:s0 + SUB], start=(kc == 0), stop=(kc == 3)),
                        reads=[("wgs", slot), ("gyT",)], writes=[("G", gb)])
                for kc in range(4):
                    P.op("pe", lambda e, kc=kc, gb=gb, slot=slot, vc2=vc2, s0=s0: e.matmul(
                        C.U[gb][:], lhsT=C.wus[slot][:, kc, vc2 * 128:(vc2 + 1) * 128],
                        rhs=gyT[:, kc, s0:s0 + SUB], start=(kc == 0), stop=(kc == 3)),
                        reads=[("wus", slot), ("gyT",)], writes=[("U", gb)])
                P.op("act", lambda e, gb=gb, vc=vc: e.activation(
                    out=C.st[gb][:], in_=C.U[gb][:], func=AF.Sigmoid, bias=C.bglu[:, 4 + vc:5 + vc]),
                    reads=[("U", gb), ("bglu",)], writes=[("st", gb)])
                P.op("dve", lambda e, gb=gb, vc=vc, s0=s0: e.scalar_tensor_tensor(
                    out=C.glu[:, vc, s0:s0 + SUB], in0=C.G[gb][:], scalar=C.bglu[:, vc:vc + 1],
                    in1=C.st[gb][:], op0=ALU.add, op1=ALU.mult),
                    reads=[("G", gb), ("st", gb), ("bglu",)], writes=[("glu", sub)])
    for sub in range(2):
        s0 = sub * SUB
        P.op("act", lambda e, s0=s0: e.activation(out=C.sq[:, 0:4, :], in_=C.glu[:, :, s0:s0 + SUB],
                                                  func=AF.Square),
             reads=[("glu", sub)], writes=[("sq",)])
        for kc in range(4):
            P.op("pe", lambda e, kc=kc: e.matmul(C.Nb[:], lhsT=C.ones[:], rhs=C.sq[:, kc, :],
                                                 start=(kc == 0), stop=(kc == 3)),
                 reads=[("sq",), ("ones",)], writes=[("Nb",)])
        P.op("act", lambda e: e.activation(out=C.rs[:], in_=C.Nb[:], func=AF.Sqrt,
                                           bias=C.epsb[:, 0:1], scale=1.0 / 512),
             reads=[("Nb",), ("epsb",)], writes=[("rs",)])
        P.op("dve", lambda e: e.reciprocal(out=C.rstd[:], in_=C.rs[:]), reads=[("rs",)], writes=[("rstd",)])
        for kc in range(4):
            P.op("dve", lambda e, kc=kc, s0=s0: e.scalar_tensor_tensor(
                out=mixT[:, kc, s0:s0 + SUB], in0=C.glu[:, kc, s0:s0 + SUB],
                scalar=C.gssm[:, kc:kc + 1], in1=C.rstd[:], op0=ALU.mult, op1=ALU.mult),
                reads=[("glu", sub), ("rstd",), ("gssm",)], writes=[("mixs", sub)])
    emit_proj_residual(P, C, I["wout"], mixT, lambda sub: [("mixs", sub), ("mixo",)])


def emit_memkv(P, C, I):
    memT = P.sb("memT_sb", [128, DK, 256], F32)
    memn = P.sb("memn", [128, DK, 256], BF16)
    C.kxT = P.sb("kxT", [128, DK, 256], BF16)
    C.Vx = P.sb("Vx", [128, 2, D], BF16)
    P.dma("sp", memT[:], I["memT"][:, :, :], reads=[], writes=[("memT",)], key=("memT",))
    P.op("act", lambda e: e.activation(out=C.sq[:, :, 0:256], in_=memT[:], func=AF.Square),
         reads=[("memT",)], writes=[("sq",)])
    for dk in range(DK):
        P.op("pe", lambda e, dk=dk: e.matmul(C.Nb[:, 0:256], lhsT=C.ones[:], rhs=C.sq[:, dk, 0:256],
                                             start=(dk == 0), stop=(dk == DK - 1)),
             reads=[("sq",), ("ones",)], writes=[("Nb",)])
    P.op("act", lambda e: e.activation(out=C.rs[:, 0:256], in_=C.Nb[:, 0:256], func=AF.Sqrt,
                                       bias=C.epsb[:, 0:1], scale=1.0 / D),
         reads=[("Nb",), ("epsb",)], writes=[("rs",)])
    P.op("dve", lambda e: e.reciprocal(out=C.rstd[:, 0:256], in_=C.rs[:, 0:256]),
         reads=[("rs",)], writes=[("rstd",)])
    for dk in range(DK):
        P.op("dve", lambda e, dk=dk: e.scalar_tensor_tensor(
            out=memn[:, dk, :], in0=memT[:, dk, :], scalar=C.gmem[:, dk:dk + 1], in1=C.rstd[:, 0:256],
            op0=ALU.mult, op1=ALU.mult), reads=[("memT",), ("rstd",), ("gmem",)], writes=[("memn",)])
    for mg in range(4):
        slot = load_wgu(P, C, I["wkv"][mg], None)
        for mc2 in range(2):
            ch = mg * 2 + mc2
            gb = C.i_g % 2
            C.i_g += 1
            for dk in range(DK):
                P.op("pe", lambda e, dk=dk, gb=gb, slot=slot, mc2=mc2: e.matmul(
                    C.G[gb][:, 0:256], lhsT=C.wgs[slot][:, dk, mc2 * 128:(mc2 + 1) * 128],
                    rhs=memn[:, dk, :], start=(dk == 0), stop=(dk == DK - 1)),
                    reads=[("wgs", slot), ("memn",)], writes=[("G", gb)])
            P.op("act", lambda e, gb=gb, ch=ch: e.copy(out=C.kxT[:, ch, :], in_=C.G[gb][:, 0:256]),
                 reads=[("G", gb)], writes=[("kxT",)])
    for mg in range(4, 8):
        slot = load_wgu(P, C, I["wkv"][mg], None)
        for blk in range(2):
            gb = C.i_g % 2
            C.i_g += 1
            for dk in range(DK):
                P.op("pe", lambda e, dk=dk, gb=gb, slot=slot, blk=blk: e.matmul(
                    C.G[gb][:, 0:256], lhsT=memn[:, dk, blk * 128:(blk + 1) * 128],
                    rhs=C.wgs[slot][:, dk, :], start=(dk == 0), stop=(dk == DK - 1)),
                    reads=[("wgs", slot), ("memn",)], writes=[("G", gb)])
            P.op("act", lambda e, gb=gb, blk=blk, mg=mg: e.copy(
                out=C.Vx[:, blk, (mg - 4) * 256:(mg - 3) * 256], in_=C.G[gb][:, 0:256]),
                reads=[("G", gb)], writes=[("Vx",)])


def emit_xattn(P, C, I):
    qxT = C.act[:, 0:8, :]
    oxT = C.act[:, 8:16, :]
    for sub in range(2):
        emit_norm(P, C, sub, C.gx, ("gx",))
    for mg in range(4):
        slot = load_wgu(P, C, I["wq"][mg], None)
        for mc2 in range(2):
            ch = mg * 2 + mc2
            for sub in range(2):
                gb = C.i_g % 2
                C.i_g += 1
                s0 = sub * SUB
                for dk in range(DK):
                    P.op("pe", lambda e, dk=dk, gb=gb, slot=slot, mc2=mc2, s0=s0: e.matmul(
                        C.G[gb][:], lhsT=C.wgs[slot][:, dk, mc2 * 128:(mc2 + 1) * 128],
                        rhs=C.xn[:, dk, s0:s0 + SUB], start=(dk == 0), stop=(dk == DK - 1)),
                        reads=[("wgs", slot), ("xn", sub)], writes=[("G", gb)])
                P.op("act", lambda e, gb=gb, ch=ch, s0=s0: e.copy(out=qxT[:, ch, s0:s0 + SUB], in_=C.G[gb][:]),
                     reads=[("G", gb)], writes=[("qxT", sub)])
    for hx in range(4):
        for sub in range(2):
            s0 = sub * SUB
            for blk in range(2):
                gb = C.i_g % 2
                C.i_g += 1
                for c2 in range(2):
                    P.op("pe", lambda e, gb=gb, hx=hx, c2=c2, blk=blk, s0=s0: e.matmul(
                        C.G[gb][:], lhsT=C.kxT[:, 2 * hx + c2, blk * 128:(blk + 1) * 128],
                        rhs=qxT[:, 2 * hx + c2, s0:s0 + SUB], start=(c2 == 0), stop=(c2 == 1)),
                        reads=[("kxT",), ("qxT", sub)], writes=[("G", gb)])
                P.op("act", lambda e, gb=gb, blk=blk: e.activation(
                    out=C.Px[blk][:], in_=C.G[gb][:], func=AF.Exp, scale=1.0 / 16),
                    reads=[("G", gb)], writes=[("Px", blk)])
            for blk in range(2):
                P.op("pe", lambda e, blk=blk: e.matmul(C.X7[:], lhsT=C.ones[:], rhs=C.Px[blk][:],
                                                       start=(blk == 0), stop=(blk == 1)),
                     reads=[("ones",), ("Px", blk)], writes=[("X7",)])
            P.op("dve", lambda e: e.reciprocal(out=C.rs[:], in_=C.X7[:]), reads=[("X7",)], writes=[("rs",)])
            for c2 in range(2):
                ub = C.i_g % 2
                C.i_g += 1
                for blk in range(2):
                    P.op("pe", lambda e, ub=ub, hx=hx, c2=c2, blk=blk: e.matmul(
                        C.U[ub][:], lhsT=C.Vx[:, blk, (2 * hx + c2) * 128:(2 * hx + c2 + 1) * 128],
                        rhs=C.Px[blk][:], start=(blk == 0), stop=(blk == 1)),
                        reads=[("Vx",), ("Px", blk)], writes=[("U", ub)])
                P.op("dve", lambda e, ub=ub, hx=hx, c2=c2, s0=s0: e.tensor_tensor(
                    out=oxT[:, 2 * hx + c2, s0:s0 + SUB], in0=C.U[ub][:], in1=C.rs[:], op=ALU.mult),
                    reads=[("U", ub), ("rs",)], writes=[("oxT", sub)])
    emit_proj_residual(P, C, I["wo"], oxT, lambda sub: [("oxT", sub)])


def build_C(mode):
    P = Prog()
    nc = P.nc
    C = Ctx()
    I = {}

    def din(name, shape, dtype=F32):
        I[name] = P.dram(name, shape, dtype, kind="ExternalInput")

    din("hT_in", [128, DK, LH])
    din("gyT", [4, 128, LH], BF16)
    din("oT", [4, 128, LH], BF16)
    din("memT", [128, DK, 256])
    din("wglu", [4, 128, 1024])
    din("b_glu", [128, 8])
    din("g_ssm", [128, 4])
    din("wout", [4, 128, 2048])
    din("g_x", [128, DK])
    din("g_mem", [128, DK])
    din("wq", [4, 128, 2048])
    din("wkv", [8, 128, 2048])
    din("wo", [4, 128, 2048])
    din("g_ffn2", [128, DK])
    din("wg2", [11, 128, 2048])
    din("wu2", [11, 128, 2048])
    din("wd2", [8, 128, FK * 128])
    if mode == "A":
        din("g_ffn1", [128, DK])
        din("g_mix", [128, DK])
        din("wg", [11, 128, 2048])
        din("wu", [11, 128, 2048])
        din("wd", [8, 128, FK * 128])
        din("win", [8, 128, 2048])
        zT_out = P.dram("zT_out", [12, 128, LH], BF16, kind="ExternalOutput")
        v_out = P.dram("v_out", [LH, 512], BF16, kind="ExternalOutput")
    else:
        din("g_final", [128, DK])
    hT_out = P.dram("hT_out", [128, DK, LH], F32, kind="ExternalOutput")
    owner_common(P, C)
    C.epsb = P.sb("epsb", [128, 1], F32)
    P.op("dve", lambda e: e.memset(C.epsb[:], EPS), writes=[("epsb",)])
    C.glu = P.sb("glu", [128, 4, NT], F32)
    C.Px = [P.sb("Px%d" % i, [128, SUB], BF16) for i in range(2)]
    C.bglu = load_small(P, C, "bglu", I["b_glu"][:, :], [128, 8])
    C.gssm = load_small(P, C, "gssm", I["g_ssm"][:, :], [128, 4])
    C.gx = load_small(P, C, "gx", I["g_x"][:, :], [128, DK])
    C.gmem = load_small(P, C, "gmem", I["g_mem"][:, :], [128, DK])
    gf2 = load_small(P, C, "gf2", I["g_ffn2"][:, :], [128, DK])
    if mode == "A":
        gf = load_small(P, C, "gf", I["g_ffn1"][:, :], [128, DK])
        gm = load_small(P, C, "gm", I["g_mix"][:, :], [128, DK])
    else:
        gfin = load_small(P, C, "gfin", I["g_final"][:, :], [128, DK])
    emit_memkv(P, C, I)
    for ti in range(LH // NT):
        t0 = ti * NT
        P.dma("sp", C.hT[:], I["hT_in"][:, :, t0:t0 + NT], reads=[],
              writes=[("hT", 0), ("hT", 1)], key=("hT",))
        emit_mixer_out(P, C, I, t0)
        emit_xattn(P, C, I)
        emit_ffn(P, C, I["wg2"], I["wu2"], I["wd2"], gf2, ("gf2",))
        if mode == "A":
            emit_ffn(P, C, I["wg"], I["wu"], I["wd"], gf, ("gf",))
            P.dma("sp", hT_out[:, :, t0:t0 + NT], C.hT[:], reads=[("hT", 0), ("hT", 1)],
                  writes=[("hT_out",)], key=("hTst",))
            emit_inproj(P, C, I["win"], gm, ("gm",), zT_out, v_out, t0)
        else:
            for sub in range(2):
                s0 = sub * SUB
                P.op("act", lambda e, s0=s0: e.activation(out=C.sq[:], in_=C.hT[:, :, s0:s0 + SUB], func=AF.Square),
                     reads=[("hT", sub)], writes=[("sq",)])
                for dk in range(DK):
                    P.op("pe", lambda e, dk=dk: e.matmul(C.Nb[:], lhsT=C.ones[:], rhs=C.sq[:, dk, :],
                                                         start=(dk == 0), stop=(dk == DK - 1)),
                         reads=[("sq",), ("ones",)], writes=[("Nb",)])
                P.op("act", lambda e: e.activation(out=C.rs[:], in_=C.Nb[:], func=AF.Sqrt,
                                                   bias=C.epsb[:, 0:1], scale=1.0 / D),
                     reads=[("Nb",), ("epsb",)], writes=[("rs",)])
                P.op("dve", lambda e: e.reciprocal(out=C.rstd[:], in_=C.rs[:]), reads=[("rs",)], writes=[("rstd",)])
                for dk in range(DK):
                    P.op("dve", lambda e, dk=dk, s0=s0: e.scalar_tensor_tensor(
                        out=C.hT[:, dk, s0:s0 + SUB], in0=C.hT[:, dk, s0:s0 + SUB],
                        scalar=gfin[:, dk:dk + 1], in1=C.rstd[:], op0=ALU.mult, op1=ALU.mult),
                        reads=[("rstd",), ("gfin",)], writes=[("hT", sub)])
            P.dma("sp", hT_out[:, :, t0:t0 + NT], C.hT[:], reads=[("hT", 0), ("hT", 1)],
                  writes=[("hT_out",)], key=("hTst",))
    P.emit()
    return nc


def lay_C_shared(inp, l, mode):
    m = {}
    m["wglu"] = lay_w_kgroups(inp["ssm_w_glu"][l])
    m["b_glu"] = np.ascontiguousarray(inp["ssm_b_glu"][l].reshape(8, 128).T)
    m["g_ssm"] = np.ascontiguousarray(inp["ssm_out_norm"][l].reshape(4, 128).T)
    m["wout"] = lay_w_kgroups(inp["w_out"][l])
    m["g_x"] = lay_gain(inp["xattn_norm"][l])
    m["g_mem"] = lay_gain(inp["mem_norm"][l])
    m["wq"] = lay_w_kgroups(inp["xattn_wq"][l])
    m["wkv"] = lay_w_kgroups(inp["xattn_wkv"][l])
    m["wo"] = lay_w_kgroups(inp["xattn_wo"][l])
    m["g_ffn2"] = lay_gain(inp["ffn2_norm"][l])
    m["wg2"] = lay_w_kgroups(inp["ffn2_w_gate"][l])
    m["wu2"] = lay_w_kgroups(inp["ffn2_w_up"][l])
    m["wd2"] = lay_wd(inp["ffn2_w_down"][l])
    if mode == "A":
        m.update(lay_A_shared(inp, l + 1))
    else:
        m["g_final"] = lay_gain(inp["final_norm"])
    return m


def lay_A_shared(inp, l):
    return dict(
        g_ffn1=lay_gain(inp["ffn1_norm"][l]), g_mix=lay_gain(inp["mix_norm"][l]),
        wg=lay_w_kgroups(inp["ffn1_w_gate"][l]), wu=lay_w_kgroups(inp["ffn1_w_up"][l]),
        wd=lay_wd(inp["ffn1_w_down"][l]), win=lay_w_kgroups(inp["w_in"][l]))


_PROGS = {}


def _prog(name):
    if name not in _PROGS:
        if name == "A":
            _PROGS[name] = build_A()
        elif name == "B":
            _PROGS[name] = build_B()
        elif name == "CA":
            _PROGS[name] = build_C("A")
        else:
            _PROGS[name] = build_C("F")
    return _PROGS[name]


def _run(name, in_maps):
    res = run_bass_kernel_spmd(_prog(name), in_maps, core_ids=list(range(NCORES)))
    return res.results


def _mixer_inputs(inp, l, resA):
    maps = []
    for c in range(NCORES):
        b, s = c // 2, c % 2
        m = lay_B_params(inp, l, s)
        m.update(lay_B_acts([resA[2 * b]["zT_out"], resA[2 * b + 1]["zT_out"]],
                            [resA[2 * b]["v_out"], resA[2 * b + 1]["v_out"]], s))
        maps.append(m)
    return maps


def _owner_inputs(inp, l, mode, resA, resB, shared):
    maps = []
    for c in range(NCORES):
        b, s = c // 2, c % 2
        m = dict(shared)
        m["hT_in"] = resA[c]["hT_out"]
        gy = np.concatenate([resB[2 * b]["gy_out"], resB[2 * b + 1]["gy_out"]], axis=0)
        gy = gy.reshape(4, 128, 32, 256).transpose(0, 1, 3, 2).reshape(4, 128, L)
        m["gyT"] = np.ascontiguousarray(gy[:, :, s * LH:(s + 1) * LH])
        o = np.concatenate([resB[2 * b]["oT_out"], resB[2 * b + 1]["oT_out"]], axis=0)
        m["oT"] = np.ascontiguousarray(o[:, :, s * LH:(s + 1) * LH])
        m["memT"] = lay_hT(inp["mem"][b])
        maps.append(m)
    return maps


def kernel(**inp):
    inp = {k: np.asarray(v) for k, v in inp.items()}
    B = inp["x"].shape[0]
    shared = lay_A_shared(inp, 0)
    maps = []
    for c in range(NCORES):
        b, s = c // 2, c % 2
        m = dict(shared)
        m["hT_in"] = lay_hT(inp["x"][b, s * LH:(s + 1) * LH])
        maps.append(m)
    resA = _run("A", maps)
    resB = _run("B", _mixer_inputs(inp, 0, resA))
    resA = _run("CA", _owner_inputs(inp, 0, "A", resA, resB, lay_C_shared(inp, 0, "A")))
    resB = _run("B", _mixer_inputs(inp, 1, resA))
    resF = _run("CF", _owner_inputs(inp, 1, "F", resA, resB, lay_C_shared(inp, 1, "F")))
    out = np.empty((B, L, D), np.float32)
    for c in range(NCORES):
        b, s = c // 2, c % 2
        out[b, s * LH:(s + 1) * LH] = unlay_hT(resF[c]["hT_out"])
    return out
```
